# Optimizing a Trainium2 kernel written in Bass

```python
import math
import jax, jax.numpy as jnp
from jax import lax
import numpy as np

D_MODEL = 2048
BATCH = 2
SEQ = 4096
DEPTH = 2

N_MIXERS = 2
MEM_LEN = 256
CHUNK = 128
A_HEAD_DIM = 128
A_WIDTH = D_MODEL
A_HEADS = A_WIDTH // A_HEAD_DIM
B_GROUP = 16
B_STATE = 64
B_WIDTH = D_MODEL
B_GROUPS = B_WIDTH // B_GROUP
DT_MIN = 1e-3
DT_MAX = 1e-1
LAM_RE_MAX = -1e-4
X_HEADS = 4
X_HEAD_DIM = D_MODEL // X_HEADS
D_FF = 4 * D_MODEL
EPS = 1e-6

kernel_name = "hybrid_gmlp_s5_memory_trunk"

F32 = jnp.float32


def rms_norm(x, g):
    xf = x.astype(F32)
    y = xf * lax.rsqrt(jnp.mean(xf * xf, axis=-1, keepdims=True) + EPS)
    return (y * g.astype(F32)).astype(x.dtype)


def layer_norm(x, g, b):
    xf = x.astype(F32)
    mu = jnp.mean(xf, axis=-1, keepdims=True)
    var = jnp.mean(jnp.square(xf - mu), axis=-1, keepdims=True)
    y = (xf - mu) * lax.rsqrt(var + EPS)
    return (y * g.astype(F32) + b.astype(F32)).astype(x.dtype)


def chunked_gmlp(h, w_in, b_in, ln_g, ln_b, w_s, b_s, w_out):
    bsz, seq, _ = h.shape
    z = jax.nn.gelu(h @ w_in + b_in, approximate=False)
    u, v = jnp.split(z, 2, axis=-1)
    v = layer_norm(v, ln_g, ln_b)
    v = v.reshape(bsz, seq // CHUNK, CHUNK, A_HEADS, A_HEAD_DIM)
    causal = jnp.tril(jnp.ones((CHUNK, CHUNK), dtype=bool))
    w = jnp.where(causal[None], w_s, jnp.zeros((), w_s.dtype))
    s = jnp.einsum('hts,bnshd->bnthd', w, v) + jnp.swapaxes(b_s, 0, 1)[None, None, :, :, None]
    s = s.reshape(bsz, seq, A_WIDTH)
    return (u * s) @ w_out


def s5_glu(h, w_in, lam_re, lam_im, log_dt, bm_re, bm_im, cm_re, cm_im, d_skip, w_out, b_out):
    bsz, seq, _ = h.shape
    u = (h @ w_in).astype(F32).reshape(bsz, seq, B_GROUPS, B_GROUP)
    lam = lax.complex(jnp.minimum(lam_re.astype(F32), LAM_RE_MAX), lam_im.astype(F32))
    dt = jnp.exp(log_dt.astype(F32))[:, None]
    lam_bar = jnp.exp(lam * dt)
    bmat = lax.complex(bm_re.astype(F32), bm_im.astype(F32))
    b_bar = ((lam_bar - 1.0) / lam)[..., None] * bmat
    bu = jnp.einsum('gpc,bsgc->sbgp', b_bar, u.astype(jnp.complex64))
    a = jnp.broadcast_to(lam_bar[None, None], (seq, 1, B_GROUPS, B_STATE))

    def combine(left, right):
        a_l, b_l = left
        a_r, b_r = right
        return a_r * a_l, a_r * b_l + b_r

    _, states = lax.associative_scan(combine, (a, bu), axis=0)
    cmat = lax.complex(cm_re.astype(F32), cm_im.astype(F32))
    y = jnp.einsum('gcp,sbgp->bsgc', cmat, states).real + d_skip.astype(F32) * u
    y = jax.nn.gelu(y.reshape(bsz, seq, B_WIDTH), approximate=False).astype(h.dtype)
    val, gate = jnp.split(y @ w_out + b_out, 2, axis=-1)
    return val * jax.nn.sigmoid(gate)


def memory_attention(h, mem_n, w_q, w_kv, w_o):
    bsz, seq, _ = h.shape
    m = mem_n.shape[1]
    q = (h @ w_q).reshape(bsz, seq, X_HEADS, X_HEAD_DIM)
    k, v = jnp.split(mem_n @ w_kv, 2, axis=-1)
    k = k.reshape(bsz, m, X_HEADS, X_HEAD_DIM)
    v = v.reshape(bsz, m, X_HEADS, X_HEAD_DIM)
    scores = jnp.einsum('bshd,bmhd->bhsm', q, k).astype(F32) * (X_HEAD_DIM ** -0.5)
    p = jax.nn.softmax(scores, axis=-1).astype(v.dtype)
    o = jnp.einsum('bhsm,bmhd->bshd', p, v).reshape(bsz, seq, X_HEADS * X_HEAD_DIM)
    return o @ w_o


def squared_relu_mlp(h, w_up, w_down):
    return jnp.square(jax.nn.relu(h @ w_up)) @ w_down


def setup_inputs(seed: int = 0) -> dict:
    key = jax.random.key(seed)
    ks = iter(jax.random.split(key, 64))

    def nrm(shape, scale):
        return scale * jax.random.normal(next(ks), shape, F32)

    n_a = len(range(0, DEPTH, N_MIXERS))
    n_b = len(range(1, DEPTH, N_MIXERS))
    x = nrm((BATCH, SEQ, D_MODEL), 1.0)
    mem = nrm((BATCH, MEM_LEN, D_MODEL), 1.0)
    g_mix = 1.0 + nrm((DEPTH, D_MODEL), 0.02)
    g_xattn = 1.0 + nrm((DEPTH, D_MODEL), 0.02)
    g_mem = 1.0 + nrm((DEPTH, D_MODEL), 0.02)
    g_ff = 1.0 + nrm((DEPTH, D_MODEL), 0.02)
    g_final = 1.0 + nrm((D_MODEL,), 0.02)
    a_w_in = nrm((n_a, D_MODEL, 2 * A_WIDTH), D_MODEL ** -0.5)
    a_b_in = nrm((n_a, 2 * A_WIDTH), 0.02)
    a_ln_g = 1.0 + nrm((n_a, A_WIDTH), 0.02)
    a_ln_b = nrm((n_a, A_WIDTH), 0.02)
    a_w_s = nrm((n_a, A_HEADS, CHUNK, CHUNK), 0.5 * CHUNK ** -0.5)
    a_b_s = 1.0 + nrm((n_a, A_HEADS, CHUNK), 0.1)
    a_w_out = nrm((n_a, A_WIDTH, D_MODEL), A_WIDTH ** -0.5)
    b_w_in = nrm((n_b, D_MODEL, B_WIDTH), D_MODEL ** -0.5)
    n_idx = jnp.arange(B_STATE, dtype=F32)
    b_lam_re = -0.5 + nrm((n_b, B_GROUPS, B_STATE), 0.01)
    b_lam_im = math.pi * n_idx + nrm((n_b, B_GROUPS, B_STATE), 0.01)
    b_log_dt = jax.random.uniform(next(ks), (n_b, B_GROUPS), F32, math.log(DT_MIN), math.log(DT_MAX))
    b_bm_re = nrm((n_b, B_GROUPS, B_STATE, B_GROUP), (2 * B_GROUP) ** -0.5)
    b_bm_im = nrm((n_b, B_GROUPS, B_STATE, B_GROUP), (2 * B_GROUP) ** -0.5)
    b_cm_re = nrm((n_b, B_GROUPS, B_GROUP, B_STATE), B_STATE ** -0.5)
    b_cm_im = nrm((n_b, B_GROUPS, B_GROUP, B_STATE), B_STATE ** -0.5)
    b_d = nrm((n_b, B_GROUPS, B_GROUP), 1.0)
    b_w_out = nrm((n_b, B_WIDTH, 2 * D_MODEL), B_WIDTH ** -0.5)
    b_b_out = nrm((n_b, 2 * D_MODEL), 0.02)
    x_w_q = nrm((DEPTH, D_MODEL, D_MODEL), D_MODEL ** -0.5)
    x_w_kv = nrm((DEPTH, D_MODEL, 2 * D_MODEL), D_MODEL ** -0.5)
    x_w_o = nrm((DEPTH, D_MODEL, D_MODEL), D_MODEL ** -0.5)
    f_w_up = nrm((DEPTH, D_MODEL, D_FF), D_MODEL ** -0.5)
    f_w_down = nrm((DEPTH, D_FF, D_MODEL), D_FF ** -0.5)
    return {
        'x': x, 'mem': mem,
        'g_mix': g_mix, 'g_xattn': g_xattn, 'g_mem': g_mem, 'g_ff': g_ff, 'g_final': g_final,
        'a_w_in': a_w_in, 'a_b_in': a_b_in, 'a_ln_g': a_ln_g, 'a_ln_b': a_ln_b,
        'a_w_s': a_w_s, 'a_b_s': a_b_s, 'a_w_out': a_w_out,
        'b_w_in': b_w_in, 'b_lam_re': b_lam_re, 'b_lam_im': b_lam_im, 'b_log_dt': b_log_dt,
        'b_bm_re': b_bm_re, 'b_bm_im': b_bm_im, 'b_cm_re': b_cm_re, 'b_cm_im': b_cm_im,
        'b_d': b_d, 'b_w_out': b_w_out, 'b_b_out': b_b_out,
        'x_w_q': x_w_q, 'x_w_kv': x_w_kv, 'x_w_o': x_w_o,
        'f_w_up': f_w_up, 'f_w_down': f_w_down,
    }


def reference(x, mem, g_mix, g_xattn, g_mem, g_ff, g_final,
              a_w_in, a_b_in, a_ln_g, a_ln_b, a_w_s, a_b_s, a_w_out,
              b_w_in, b_lam_re, b_lam_im, b_log_dt, b_bm_re, b_bm_im, b_cm_re, b_cm_im,
              b_d, b_w_out, b_b_out,
              x_w_q, x_w_kv, x_w_o, f_w_up, f_w_down):
    for i in range(DEPTH):
        j = i // N_MIXERS
        h = rms_norm(x, g_mix[i])
        if i % N_MIXERS == 0:
            x = x + chunked_gmlp(h, a_w_in[j], a_b_in[j], a_ln_g[j], a_ln_b[j],
                                 a_w_s[j], a_b_s[j], a_w_out[j])
        else:
            x = x + s5_glu(h, b_w_in[j], b_lam_re[j], b_lam_im[j], b_log_dt[j],
                           b_bm_re[j], b_bm_im[j], b_cm_re[j], b_cm_im[j],
                           b_d[j], b_w_out[j], b_b_out[j])
        x = x + memory_attention(rms_norm(x, g_xattn[i]), rms_norm(mem, g_mem[i]),
                                 x_w_q[i], x_w_kv[i], x_w_o[i])
        x = x + squared_relu_mlp(rms_norm(x, g_ff[i]), f_w_up[i], f_w_down[i])
    return rms_norm(x, g_final)
```

```python
import numpy as np
import concourse.bass as bass
import concourse.mybir as mybir
from concourse.bass_utils import run_bass_kernel_spmd

F32 = mybir.dt.float32
BF16 = mybir.dt.bfloat16
AF = mybir.ActivationFunctionType
ALU = mybir.AluOpType
AX = mybir.AxisListType

D = 2048
NTOK = 1024
NCH = 16
EPS = 1e-6
DBG = None
T1 = 8
NBLK = NTOK // T1


class _Op:
    __slots__ = ("eng", "idx", "emit", "waits", "signal", "dma_sem", "dma_val", "sigval")

    def __init__(self, eng, emit):
        self.eng = eng
        self.emit = emit
        self.waits = []
        self.signal = False
        self.dma_sem = None
        self.dma_val = 0
        self.sigval = 0


class Sched:
    ENGS = ("pe", "dve", "act", "pool", "sp")

    def __init__(self):
        self.ops = {e: [] for e in self.ENGS}
        self.last_w = {}
        self.readers = {}
        self.seen = {e: {} for e in self.ENGS}
        self.seen_dma = {e: {} for e in self.ENGS}
        self.dma_counts = {}

    def _dep(self, op, tok):
        if tok is None:
            return
        if tok[0] == "op":
            src = tok[1]
            if src.eng == op.eng and op.eng == "pe":
                return
            if self.seen[op.eng].get(src.eng, -1) >= src.idx:
                return
            self.seen[op.eng][src.eng] = src.idx
            src.signal = True
            op.waits.append(tok)
        else:
            _, sname, val = tok
            if self.seen_dma[op.eng].get(sname, 0) >= val:
                return
            self.seen_dma[op.eng][sname] = val
            op.waits.append(tok)

    def add(self, eng, emit, reads=(), writes=(), dma_sem=None):
        op = _Op(eng, emit)
        op.idx = len(self.ops[eng])
        for k in reads:
            self._dep(op, self.last_w.get(k))
        for k in writes:
            self._dep(op, self.last_w.get(k))
            for r in reversed(self.readers.get(k, ())):
                self._dep(op, r)
        if dma_sem is not None:
            self.dma_counts[dma_sem] = self.dma_counts.get(dma_sem, 0) + (1 if dma_sem == "cc" else 16)
            op.dma_sem = dma_sem
            op.dma_val = self.dma_counts[dma_sem]
            tok = ("dma", dma_sem, op.dma_val)
        else:
            tok = ("op", op)
        for k in reads:
            self.readers.setdefault(k, []).append(tok)
        for k in writes:
            self.last_w[k] = tok
            self.readers[k] = []
        self.ops[eng].append(op)
        return op

    def close_group(self, sem):
        total = self.dma_counts.get(sem, 0)
        for k, tok in list(self.last_w.items()):
            if tok[0] == "dma" and tok[1] == sem:
                self.last_w[k] = ("dma", sem, total)

    def emit_all(self, nc, block_engines, sems):
        for e in self.ENGS:
            c = 0
            for op in self.ops[e]:
                if op.signal and op.dma_sem is None:
                    c += 1
                    op.sigval = c
        for e in self.ENGS:
            if not self.ops[e]:
                continue
            eng = block_engines[e]
            for op in self.ops[e]:
                for w in op.waits:
                    if w[0] == "op":
                        src = w[1]
                        if src.dma_sem is not None:
                            eng.wait_ge(sems[src.dma_sem], src.dma_val)
                        else:
                            eng.wait_ge(sems["eng_" + src.eng], src.sigval)
                    else:
                        eng.wait_ge(sems[w[1]], w[2])
                ins = op.emit(eng)
                if op.dma_sem is not None:
                    ins.then_inc(sems[op.dma_sem], 16)
                elif op.signal:
                    ins.then_inc(sems["eng_" + e], 1)


PARAM_SPECS = [
    ("x", [NTOK, D]), ("mem", [256, D]),
    ("g_mix", [2, D]), ("g_xattn", [2, D]), ("g_mem", [2, D]), ("g_ff", [2, D]), ("g_final", [D]),
    ("a_w_in", [1, D, 2 * D]), ("a_b_in", [1, 2 * D]), ("a_ln_g", [1, D]), ("a_ln_b", [1, D]),
    ("a_w_s", [1, 16, 128, 128]), ("a_b_s", [1, 16, 128]), ("a_w_out", [1, D, D]),
    ("b_w_in", [1, D, D]), ("b_lam_re", [1, 128, 64]), ("b_lam_im", [1, 128, 64]), ("b_log_dt", [1, 128]),
    ("b_bm_re", [1, 128, 64, 16]), ("b_bm_im", [1, 128, 64, 16]),
    ("b_cm_re", [1, 128, 16, 64]), ("b_cm_im", [1, 128, 16, 64]),
    ("b_d", [1, 128, 16]), ("b_w_out", [1, D, 2 * D]), ("b_b_out", [1, 2 * D]),
    ("x_w_q", [2, D, D]), ("x_w_kv", [2, D, 2 * D]), ("x_w_o", [2, D, D]),
    ("f_w_up", [2, D, 4 * D]), ("f_w_down", [2, 4 * D, D]),
    ("c_ident", [128, 128]), ("c_tril", [128, 128]), ("c_sel", [128, 24]), ("c_mk", [128, 12]),
    ("c_selb", [64, 128]),
]

PV = {}
_pv_n = 0


def _pv(name, n=1):
    global _pv_n
    PV[name] = _pv_n
    _pv_n += n


for _n in ("g_mix0", "g_mix1", "g_xattn0", "g_xattn1", "g_ff0", "g_ff1", "g_final",
           "a_b_in_u", "a_b_in_v", "a_ln_g", "a_ln_b", "b_b_val", "b_b_gate"):
    _pv(_n)
NPV = _pv_n


def build_program(stage=99, ncores=8):
    nc = bass.Bass("TRN2", target_bir_lowering=False)
    dr = {}
    for name, shape in PARAM_SPECS:
        dr[name] = nc.dram_tensor(name, shape, F32, kind="ExternalInput").ap()
    out_d = nc.dram_tensor("out", [NTOK, D], F32, kind="ExternalOutput").ap()
    WXD_t = nc.dram_tensor("s5_wxd", [2048, 8 * 2 * 128], BF16)
    WCD_t = nc.dram_tensor("s5_wcd", [64, 128, 8 * 2 * 32], BF16)
    KD_t = nc.dram_tensor("s5_kd", [2048, 8 * 128], BF16)
    XB_t = nc.dram_tensor("s5_xb", [128, 128], F32)
    GATH_t = nc.dram_tensor("s5_gath", [1024, 128], F32)
    WXD, WCD, KD, XB, GATH = WXD_t.ap(), WCD_t.ap(), KD_t.ap(), XB_t.ap(), GATH_t.ap()

    S = Sched()
    import contextlib
    es = contextlib.ExitStack()
    with es:
        ARENA_W = 53000
        arena = es.enter_context(nc.sbuf_tensor("arena", [128, ARENA_W], F32))
        psum = [es.enter_context(nc.psum_tensor("ps%d" % i, [128, 512], F32)) for i in range(8)]
        sem_names = ["eng_" + e for e in Sched.ENGS] + ["wb%d" % i for i in range(4)] + \
                    ["par", "xin0", "xin1", "xout0", "xout1", "misc0", "misc1", "misc2", "misc3",
                     "s5a", "s5b", "s5ck", "s5cw", "s5cx", "s5k0", "s5k1", "s5x0", "s5x1", "s5w0", "s5w1", "cc",
                     "wbwCT0", "wbwCT1", "wbwHT0", "wbwHT1", "wbwBT0", "wbwBT1"]
        sems = {n: es.enter_context(nc.semaphore(n)) for n in sem_names}

        off = [0]

        def carve(words):
            a = off[0]
            off[0] += words
            assert off[0] <= ARENA_W, (off[0], ARENA_W)
            return a

        def f32v(a, words):
            return arena[:, a:a + words]

        def bf16v(a, words):
            return arena[:, a:a + words].bitcast(BF16)

        a_xt = carve(16 * 1024)
        XT = f32v(a_xt, 16 * 1024).rearrange("p (j t) -> p j t", j=16)
        a_wb = carve(4 * 1024)
        WB = [bf16v(a_wb + i * 1024, 1024).rearrange("p (k c) -> p k c", k=16) for i in range(4)]
        a_par = carve(NPV * 16)
        PAR = f32v(a_par, NPV * 16).rearrange("p (v j) -> p v j", j=16)
        a_id = carve(128)
        IDF = f32v(a_id, 128)
        a_idb = carve(64)
        IDB = bf16v(a_idb, 64)
        a_ones = carve(64)
        ONESB = bf16v(a_ones, 64)
        a_cst = carve(8)
        CST = f32v(a_cst, 8)
        a_sel = carve(24)
        SEL = f32v(a_sel, 24)
        a_mk = carve(12)
        MK = f32v(a_mk, 12)
        a_selb = carve(128)
        SELB = f32v(a_selb, 128)
        a_coef = carve(6 * 64)
        COEF = f32v(a_coef, 6 * 64).rearrange("p (k q) -> p k q", k=6)
        a_ha = carve(8192)
        a_hb = carve(8192)
        a_hc = carve(8192)
        HT = bf16v(a_ha, 8192).rearrange("p (j t) -> p j t", j=16)
        BT = bf16v(a_hb, 8192).rearrange("p (j t) -> p j t", j=16)
        CT = bf16v(a_hc, 8192).rearrange("p (j t) -> p j t", j=16)
        a_misc = carve(ARENA_W - off[0])
        MISC_W = ARENA_W - a_misc

        def pap(name, j):
            return PAR[:, PV[name], j:j + 1]

        def dma(q, out, in_, sem, reads=(), writes=(), **kw):
            return S.add(q, lambda e, o=out, i=in_, k=kw: e.dma_start(out=o, in_=i, **k),
                         reads=reads, writes=writes, dma_sem=sem)

        def mm(out, lhsT, rhs, start, stop, reads=(), writes=(), **kw):
            return S.add("pe", lambda e, o=out, l=lhsT, r=rhs, a=start, b=stop, k=kw:
                         e.matmul(o, lhsT=l, rhs=r, start=a, stop=b, **k), reads=reads, writes=writes)

        def tr(out, in_, ident, reads=(), writes=()):
            return S.add("pe", lambda e, o=out, i=in_, d=ident: e.transpose(o, i, d),
                         reads=reads, writes=writes)

        def act(out, in_, func, reads=(), writes=(), **kw):
            return S.add("act", lambda e, o=out, i=in_, f=func, k=kw: e.activation(out=o, in_=i, func=f, **k),
                         reads=reads, writes=writes)

        def tt(eng, out, in0, in1, op, reads=(), writes=()):
            return S.add(eng, lambda e, o=out, a=in0, b=in1, p=op: e.tensor_tensor(out=o, in0=a, in1=b, op=p),
                         reads=reads, writes=writes)

        def ts(eng, out, in0, s1, s2, op0, op1=None, reads=(), writes=()):
            if op1 is None:
                return S.add(eng, lambda e, o=out, a=in0, x=s1, p=op0:
                             e.tensor_scalar(out=o, in0=a, scalar1=x, scalar2=None, op0=p),
                             reads=reads, writes=writes)
            return S.add(eng, lambda e, o=out, a=in0, x=s1, y=s2, p=op0, q=op1:
                         e.tensor_scalar(out=o, in0=a, scalar1=x, scalar2=y, op0=p, op1=q),
                         reads=reads, writes=writes)

        def stt(eng, out, in0, scalar, in1, op0, op1, reads=(), writes=()):
            return S.add(eng, lambda e, o=out, a=in0, s=scalar, b=in1, p=op0, q=op1:
                         e.scalar_tensor_tensor(out=o, in0=a, scalar=s, in1=b, op0=p, op1=q),
                         reads=reads, writes=writes)

        def cp(eng, out, in_, reads=(), writes=()):
            if eng == "act":
                return S.add("act", lambda e, o=out, i=in_: e.copy(out=o, in_=i), reads=reads, writes=writes)
            return S.add(eng, lambda e, o=out, i=in_: e.tensor_copy(out=o, in_=i), reads=reads, writes=writes)

        def memset(eng, ap, val, writes=()):
            return S.add(eng, lambda e, a=ap, v=val: e.memset(a, v), writes=writes)

        def psb(bank):
            return psum[bank][:, :]

        def psbf(bank):
            return psum[bank][:, :].bitcast(BF16)

        PK = lambda b: ("ps", b)

        wb_i = [0]
        lin_bank = [0]

        def next_wb():
            s = wb_i[0] % 4
            wb_i[0] += 1
            return s

        def next_lin_bank():
            b = lin_bank[0] % 4
            lin_bank[0] += 1
            return b

        dma("sp", IDF, dr["c_ident"][:, :], "par", writes=["IDF"])
        dma("sp", SEL, dr["c_sel"][:, :], "par", writes=["SEL"])
        dma("sp", MK, dr["c_mk"][:, :], "par", writes=["MK"])
        dma("sp", SELB[0:64, :], dr["c_selb"][:, :], "par", writes=["SELB"])

        def load_pvec(name, src1d):
            dma("sp", PAR[:, PV[name], :], src1d.rearrange("(j p) -> p j", p=128), "par",
                writes=["PAR"], allow_slow_non_contiguous=True)

        load_pvec("g_mix0", dr["g_mix"][0]); load_pvec("g_mix1", dr["g_mix"][1])
        load_pvec("g_xattn0", dr["g_xattn"][0]); load_pvec("g_xattn1", dr["g_xattn"][1])
        load_pvec("g_ff0", dr["g_ff"][0]); load_pvec("g_ff1", dr["g_ff"][1])
        load_pvec("g_final", dr["g_final"])
        load_pvec("a_b_in_u", dr["a_b_in"][0, 0:D]); load_pvec("a_b_in_v", dr["a_b_in"][0, D:2 * D])
        load_pvec("a_ln_g", dr["a_ln_g"][0]); load_pvec("a_ln_b", dr["a_ln_b"][0])
        load_pvec("b_b_val", dr["b_b_out"][0, 0:D]); load_pvec("b_b_gate", dr["b_b_out"][0, D:2 * D])
        S.close_group("par")
        cp("dve", IDB, IDF, reads=["IDF"], writes=["IDB"])
        memset("dve", ONESB, 1.0, writes=["ONESB"])
        memset("dve", CST[:, 0:1], EPS, writes=["CST"])
        memset("dve", CST[:, 1:2], 0.0, writes=["CST"])
        memset("dve", CST[:, 2:3], -float(np.pi), writes=["CST"])

        class Buf:
            def __init__(self, ap, k):
                self.ap = ap
                self.k = k

        class Misc:
            def __init__(self):
                self.o = 0

            def _alloc(self, words):
                words = ((words + 255) // 256) * 256
                a = self.o
                self.o += words
                assert self.o <= MISC_W, (self.o, MISC_W)
                return a_misc + a, [("M", g) for g in range(a // 256, (a + words) // 256)], words

            def f32(self, words):
                a, k, w = self._alloc(words)
                return Buf(f32v(a, words), k)

            def bf16(self, elems):
                a, k, w = self._alloc((elems + 1) // 2)
                return Buf(bf16v(a, (elems + 1) // 2), k)

        def load_x():
            m = Misc()
            xin = [m.f32(2048), m.f32(2048)]
            for tt_ in range(8):
                b = tt_ % 2
                dma("sp", xin[b].ap, dr["x"][tt_ * 128:(tt_ + 1) * 128, :], "xin%d" % b, writes=xin[b].k)
                for g4 in range(4):
                    bank = 4 + (g4 % 2) + 2 * (tt_ % 2)
                    for jj in range(4):
                        j = g4 * 4 + jj
                        tr(psb(bank)[:, jj * 128:(jj + 1) * 128], xin[b].ap[:, j * 128:(j + 1) * 128], IDF,
                           reads=xin[b].k + ["IDF"], writes=[PK(bank)])
                    eng = "dve" if g4 % 2 == 0 else "act"
                    cp(eng, XT[:, g4 * 4:g4 * 4 + 4, tt_ * 128:(tt_ + 1) * 128],
                       psb(bank).rearrange("p (a t) -> p a t", a=4),
                       reads=[PK(bank)], writes=[("XT", j_) for j_ in range(g4 * 4, g4 * 4 + 4)])

        def rms_stats(m, srcs, key_fn, scratch_key):
            sq = [m.bf16(512) for _ in range(3)]
            rstd = [m.f32(512), m.f32(512)]
            for half in range(2):
                bank = 4 + half
                for j in range(16):
                    s = sq[j % 3]
                    act(s.ap, srcs(j, half), AF.Square, reads=[key_fn(j)], writes=s.k)
                    mm(psb(bank), ONESB, s.ap, j == 0, j == 15,
                       reads=s.k + ["ONESB"], writes=[PK(bank)])
                act(rstd[half].ap, psb(bank), AF.Sqrt, reads=[PK(bank), "CST"], writes=rstd[half].k,
                    bias=CST[:, 0:1], scale=1.0 / D)
                S.add("dve", lambda e, o=rstd[half].ap: e.reciprocal(out=o, in_=o),
                      reads=rstd[half].k, writes=rstd[half].k)
            return rstd

        def rmsnorm_to(dst, dst_name, gname):
            m = Misc()
            rstd = rms_stats(m, lambda j, h: XT[:, j, h * 512:(h + 1) * 512], lambda j: ("XT", j), "rn")
            for half in range(2):
                for j in range(16):
                    stt("dve", dst[:, j, half * 512:(half + 1) * 512], XT[:, j, half * 512:(half + 1) * 512],
                        pap(gname, j), rstd[half].ap, ALU.mult, ALU.mult,
                        reads=[("XT", j), "PAR"] + rstd[half].k, writes=[(dst_name, j)])

        wide_i = [0]
        WIDE = {"CT": a_hc, "HT": a_ha, "BT": a_hb}

        def linear(src, src_name, w2d, n_out, evac, col0=0, row0=0, wide=None):
            if wide is not None:
                assert n_out % 4 == 0
                base = WIDE[wide]
                for blk in range(n_out // 4):
                    slot = wide_i[0] % 2
                    wide_i[0] += 1
                    wbw = bf16v(base + slot * 4096, 4096).rearrange("p (k c) -> p k c", k=16)
                    wk = [(wide, j) for j in range(8 * slot, 8 * slot + 8)]
                    c0 = col0 + blk * 512
                    dma("pool", wbw, w2d[row0:row0 + D, c0:c0 + 512].rearrange("(k p) c -> p k c", p=128),
                        "wbw%s%d" % (wide, slot), writes=wk)
                    for mm_ in range(4):
                        m_ = blk * 4 + mm_
                        for half in range(2):
                            bank = next_lin_bank()
                            for k in range(16):
                                mm(psb(bank), wbw[:, k, mm_ * 128:(mm_ + 1) * 128],
                                   src[:, k, half * 512:(half + 1) * 512], k == 0, k == 15,
                                   reads=wk + [(src_name, k)], writes=[PK(bank)])
                            evac(m_, half, bank)
                return
            for m_ in range(n_out):
                slot = next_wb()
                c0 = col0 + m_ * 128
                dma("pool", WB[slot],
                    w2d[row0:row0 + D, c0:c0 + 128].rearrange("(k p) c -> p k c", p=128),
                    "wb%d" % slot, writes=[("WB", slot)])
                for half in range(2):
                    bank = next_lin_bank()
                    for k in range(16):
                        mm(psb(bank), WB[slot][:, k, :], src[:, k, half * 512:(half + 1) * 512], k == 0, k == 15,
                           reads=[("WB", slot), (src_name, k)], writes=[PK(bank)])
                    evac(m_, half, bank)

        def evac_resid(m_, half, bank):
            tt("dve", XT[:, m_, half * 512:(half + 1) * 512], XT[:, m_, half * 512:(half + 1) * 512], psb(bank),
               ALU.add, reads=[PK(bank), ("XT", m_)], writes=[("XT", m_)])

        def ffn(i):
            rmsnorm_to(HT, "HT", "g_ff%d" % i)
            m = Misc()
            tmp = [m.f32(512), m.f32(512)]
            tcnt = [0]
            for fb in range(4):
                A, An = (BT, "BT")

                def evac_up(m_, half, bank, A=A, An=An):
                    t = tcnt[0] % 2
                    tcnt[0] += 1
                    act(tmp[t].ap, psb(bank), AF.Relu, reads=[PK(bank)], writes=tmp[t].k)
                    tt("dve", A[:, m_, half * 512:(half + 1) * 512], tmp[t].ap, tmp[t].ap, ALU.mult,
                       reads=tmp[t].k, writes=[(An, m_)])

                linear(HT, "HT", dr["f_w_up"][i], 16, evac_up, col0=fb * D, wide="CT")
                linear(A, An, dr["f_w_down"][i], 16, evac_resid, row0=fb * D, wide="CT")

        def xattn(i):
            rmsnorm_to(HT, "HT", "g_xattn%d" % i)
            scl = float(512 ** -0.5)

            def evac_q(m_, half, bank):
                S.add("act", lambda e, o=BT[:, m_, half * 512:(half + 1) * 512], b=bank: e.mul(out=o, in_=psb(b), mul=scl),
                      reads=[PK(bank)], writes=[("BT", m_)])

            linear(HT, "HT", dr["x_w_q"][i], 16, evac_q, wide="CT")
            m = Misc()
            KTb = m.bf16(16 * 256)
            KT = KTb.ap.rearrange("p (j m) -> p j m", j=16)
            Vb = m.bf16(2 * 2048)
            V = Vb.ap.rearrange("p (r c) -> p r c", r=2)
            smallb = m.f32(256)
            small = smallb.ap
            E = [m.f32(256), m.f32(256)]
            Pb = [m.bf16(512), m.bf16(512)]
            PTb = [m.bf16(1024), m.bf16(1024)]
            MEM = f32v(a_hc, 4096).rearrange("p (r c) -> p r c", r=2)
            GM = f32v(a_hc + 4096, 2048)
            MNT = bf16v(a_hc + 6144, 2048).rearrange("p (j m) -> p j m", j=16)
            MNB = bf16v(a_ha, 2048).rearrange("p (r c) -> p r c", r=2)
            ctk = [("CT", j) for j in range(16)]
            htk = [("HT", j) for j in range(16)]
            dma("sp", MEM, dr["mem"].rearrange("(r p) c -> p r c", p=128), "misc0", writes=ctk)
            dma("sp", GM, dr["g_mem"][i:i + 1, :].partition_broadcast(128), "misc1", writes=ctk)
            SQT = f32v(a_hc + 6144, 2048)
            for r in range(2):
                act(SQT, MEM[:, r, :], AF.Square, reads=ctk, writes=ctk)
                S.add("dve", lambda e, o=small[:, r:r + 1]: e.reduce_sum(out=o, in_=SQT, axis=AX.X),
                      reads=ctk, writes=smallb.k)
            act(small[:, 2:4], small[:, 0:2], AF.Sqrt, reads=smallb.k + ["CST"], writes=smallb.k,
                bias=CST[:, 0:1], scale=1.0 / D)
            S.add("dve", lambda e: e.reciprocal(out=small[:, 4:6], in_=small[:, 2:4]),
                  reads=smallb.k, writes=smallb.k)
            for r in range(2):
                stt("dve", MNB[:, r, :], MEM[:, r, :], small[:, 4 + r:5 + r], GM, ALU.mult, ALU.mult,
                    reads=ctk + smallb.k, writes=htk)
            for g4 in range(4):
                bank = 4 + g4 % 2
                for jj in range(4):
                    j = g4 * 4 + jj
                    for r in range(2):
                        tr(psbf(bank)[:, (jj * 2 + r) * 128:(jj * 2 + r + 1) * 128],
                           MNB[:, r, j * 128:(j + 1) * 128], IDB, reads=htk + ["IDB"], writes=[PK(bank)])
                cp("dve", MNT[:, g4 * 4:g4 * 4 + 4, :], psbf(bank).rearrange("p (a m) -> p a m", a=4),
                   reads=[PK(bank)], writes=ctk)
            if DBG == "xm":
                allk = ctk + htk + smallb.k
                cp("dve", XT[:, 0, 0:16], small[:, 0:16], reads=allk, writes=[("XT", 0)])
                cp("dve", XT[:, 1, 0:256], MNT[:, 0, :], reads=allk, writes=[("XT", 1)])
                cp("dve", XT[:, 2, :], GM[:, 0:1024], reads=allk, writes=[("XT", 2)])
                cp("dve", XT[:, 3, :], MEM[:, 0, 0:1024], reads=allk, writes=[("XT", 3)])
                cp("dve", XT[:, 4, :], MNB[:, 0, 0:1024], reads=allk, writes=[("XT", 4)])
                raise _Stop()
            wkv = dr["x_w_kv"][i]
            for m_ in range(16):
                slot = next_wb()
                dma("pool", WB[slot], wkv[:, m_ * 128:(m_ + 1) * 128].rearrange("(k p) c -> p k c", p=128),
                    "wb%d" % slot, writes=[("WB", slot)])
                bank = next_lin_bank()
                for k in range(16):
                    mm(psb(bank)[:, 0:256], WB[slot][:, k, :], MNT[:, k, :], k == 0, k == 15,
                       reads=[("WB", slot)] + ctk, writes=[PK(bank)])
                cp("act", KT[:, m_, :], psb(bank)[:, 0:256], reads=[PK(bank)], writes=KTb.k)
            for m_ in range(16):
                slot = next_wb()
                dma("pool", WB[slot], wkv[:, D + m_ * 128:D + (m_ + 1) * 128].rearrange("(k p) c -> p k c", p=128),
                    "wb%d" % slot, writes=[("WB", slot)])
                bank = next_lin_bank()
                for r in range(2):
                    for k in range(16):
                        mm(psb(bank)[:, r * 128:(r + 1) * 128], MNT[:, k, r * 128:(r + 1) * 128], WB[slot][:, k, :],
                           k == 0, k == 15, reads=[("WB", slot)] + ctk, writes=[PK(bank)])
                cp("act", V[:, :, m_ * 128:(m_ + 1) * 128], psb(bank)[:, 0:256].rearrange("p (r c) -> p r c", r=2),
                   reads=[PK(bank)], writes=Vb.k)
            dbg_at("xq", BT, "BT")
            if DBG == "xk":
                for j in range(16):
                    cp("dve", XT[:, j, 0:256], KT[:, j, :], reads=KTb.k, writes=[("XT", j)])
                raise _Stop()
            if DBG == "xv":
                for r in range(2):
                    cp("dve", XT[:, 0:2, r * 512:(r + 1) * 512].rearrange("p a t -> p (a t)"), V[:, r, 0:1024], reads=Vb.k, writes=[("XT", 0), ("XT", 1)])
                raise _Stop()
            it = 0
            for h in range(4):
                for half in range(2):
                    pi = (h * 2 + half) % 2
                    ptb = 6 + pi
                    pt = PTb[pi].ap.rearrange("p (r t) -> p r t", r=2)
                    ptk = PTb[pi].k
                    for t4 in range(4):
                        t0 = half * 512 + t4 * 128
                        sb = 4 + it % 2
                        e_ = it % 2
                        it += 1
                        for kk in range(4):
                            mm(psb(sb)[:, 0:256], BT[:, 4 * h + kk, t0:t0 + 128], KT[:, 4 * h + kk, :], kk == 0, kk == 3,
                               reads=[("BT", 4 * h + kk)] + KTb.k, writes=[PK(sb)])
                        sm = small[:, 8 + 4 * e_:12 + 4 * e_]
                        smk = [("xa_sm", e_)]
                        S.add("dve", lambda e, o=sm[:, 0:1], b=sb: e.reduce_max(out=o, in_=psb(b)[:, 0:256], axis=AX.X),
                              reads=[PK(sb)], writes=smk)
                        ts("dve", sm[:, 1:2], sm[:, 0:1], -1.0, None, ALU.mult, reads=smk, writes=smk)
                        act(E[e_].ap, psb(sb)[:, 0:256], AF.Exp, reads=[PK(sb)] + smk, writes=E[e_].k,
                            bias=sm[:, 1:2], scale=1.0)
                        S.add("dve", lambda e, o=sm[:, 2:3], a=E[e_].ap: e.reduce_sum(out=o, in_=a, axis=AX.X),
                              reads=E[e_].k, writes=smk)
                        S.add("dve", lambda e, o=sm[:, 3:4], a=sm[:, 2:3]: e.reciprocal(out=o, in_=a),
                              reads=smk, writes=smk)
                        ts("dve", Pb[e_].ap[:, 0:256], E[e_].ap, sm[:, 3:4], None, ALU.mult, reads=E[e_].k + smk,
                           writes=Pb[e_].k)
                        for r in range(2):
                            tr(psbf(ptb)[:, r * 512 + t4 * 128: r * 512 + (t4 + 1) * 128],
                               Pb[e_].ap[:, r * 128:(r + 1) * 128],
                               IDB, reads=Pb[e_].k + ["IDB"], writes=[PK(ptb)])
                    cp("act", pt, psbf(ptb).rearrange("p (r t) -> p r t", r=2), reads=[PK(ptb)], writes=ptk)
                    for kk in range(4):
                        bank = next_lin_bank()
                        for r in range(2):
                            mm(psb(bank), V[:, r, (4 * h + kk) * 128:(4 * h + kk + 1) * 128], pt[:, r, :], r == 0, r == 1,
                               reads=Vb.k + ptk, writes=[PK(bank)])
                        cp("dve", CT[:, 4 * h + kk, half * 512:(half + 1) * 512], psb(bank), reads=[PK(bank)],
                           writes=[("CT", 4 * h + kk)])
            dbg_at("xo", CT, "CT")
            linear(CT, "CT", dr["x_w_o"][i], 16, evac_resid, wide="HT")

        def gmlp():
            rmsnorm_to(HT, "HT", "g_mix0")
            dbg_at("h", HT, "HT")
            m = Misc()
            BSb = m.f32(2048)
            BS = BSb.ap.rearrange("p (h t) -> p h t", h=16)
            WTb = m.bf16(2048)
            WT = WTb.ap.rearrange("p (h t) -> p h t", h=16)
            WN = f32v(a_hb, 2048).rearrange("p (h s) -> p h s", h=16)
            WNB = bf16v(a_hb + 2048, 1024).rearrange("p (h s) -> p h s", h=16)
            TRIL = f32v(a_hb + 3072, 128)
            btk = [("BT", j) for j in range(16)]
            dma("sp", WN, dr["a_w_s"][0].rearrange("h t s -> t h s"), "misc0", writes=btk)
            dma("sp", TRIL, dr["c_tril"][:, :], "misc1", writes=btk)
            dma("sp", BSb.ap, dr["a_b_s"][0:1].rearrange("o h t -> o (h t)").partition_broadcast(128), "misc2",
                writes=BSb.k)
            for h in range(16):
                tt("dve", WNB[:, h, :], WN[:, h, :], TRIL, ALU.mult, reads=btk, writes=btk)
            for g4 in range(2):
                bank = 4 + g4
                for hh in range(8):
                    h = g4 * 8 + hh
                    tr(psbf(bank)[:, hh * 128:(hh + 1) * 128], WNB[:, h, :], IDB, reads=btk + ["IDB"], writes=[PK(bank)])
                cp("dve", WT[:, g4 * 8:(g4 + 1) * 8, :], psbf(bank).rearrange("p (h t) -> p h t", h=8),
                   reads=[PK(bank)], writes=WTb.k)

            def evac_u(m_, half, bank):
                act(BT[:, m_, half * 512:(half + 1) * 512], psb(bank), AF.Gelu, reads=[PK(bank), "PAR"],
                    writes=[("BT", m_)], bias=pap("a_b_in_u", m_), scale=1.0)

            def evac_v(m_, half, bank):
                act(CT[:, m_, half * 512:(half + 1) * 512], psb(bank), AF.Gelu, reads=[PK(bank), "PAR"],
                    writes=[("CT", m_)], bias=pap("a_b_in_v", m_), scale=1.0)

            linear(HT, "HT", dr["a_w_in"][0], 16, evac_v, col0=D)
            linear(HT, "HT", dr["a_w_in"][0], 16, evac_u, col0=0)
            dbg_at("u", BT, "BT")
            dbg_at("v", CT, "CT")
            sq = [m.bf16(512) for _ in range(3)]
            mean = [m.f32(512), m.f32(512)]
            rstd = [m.f32(512), m.f32(512)]
            tmpf = [m.f32(512), m.f32(512)]
            for half in range(2):
                hs = slice(half * 512, (half + 1) * 512)
                b_sum, b_sq = 4 + 2 * half, 5 + 2 * half
                for j in range(16):
                    s_ = sq[j % 3]
                    mm(psb(b_sum), ONESB, CT[:, j, hs], j == 0, j == 15, reads=[("CT", j), "ONESB"], writes=[PK(b_sum)])
                    act(s_.ap, CT[:, j, hs], AF.Square, reads=[("CT", j)], writes=s_.k)
                    mm(psb(b_sq), ONESB, s_.ap, j == 0, j == 15, reads=s_.k + ["ONESB"], writes=[PK(b_sq)])
                mn, rs, tf = mean[half], rstd[half], tmpf[half]
                ts("dve", mn.ap, psb(b_sum), 1.0 / D, None, ALU.mult, reads=[PK(b_sum)], writes=mn.k)
                tt("dve", tf.ap, mn.ap, mn.ap, ALU.mult, reads=mn.k, writes=tf.k)
                stt("dve", rs.ap, psb(b_sq), 1.0 / D, tf.ap, ALU.mult, ALU.subtract,
                    reads=[PK(b_sq)] + tf.k, writes=rs.k)
                act(rs.ap, rs.ap, AF.Sqrt, reads=rs.k + ["CST"], writes=rs.k, bias=CST[:, 0:1], scale=1.0)
                S.add("dve", lambda e, o=rs.ap: e.reciprocal(out=o, in_=o), reads=rs.k, writes=rs.k)
                for j in range(16):
                    tt("dve", tf.ap, CT[:, j, hs], mn.ap, ALU.subtract, reads=[("CT", j)] + mn.k, writes=tf.k)
                    tt("pool", tf.ap, tf.ap, rs.ap, ALU.mult, reads=tf.k + rs.k, writes=tf.k)
                    act(CT[:, j, hs], tf.ap, AF.Identity, reads=tf.k + ["PAR"], writes=[("CT", j)],
                        bias=pap("a_ln_b", j), scale=pap("a_ln_g", j))
            dbg_at("vn", CT, "CT")
            VN = HT.rearrange("p j t -> p (j t)").rearrange("p (n c) -> p n c", n=8)
            htk = [("HT", j) for j in range(16)]
            cnt = 0
            for n in range(8):
                for g in range(2):
                    bank = 4 + cnt % 2
                    cnt += 1
                    for hh in range(8):
                        h = g * 8 + hh
                        tr(psbf(bank)[:, hh * 128:(hh + 1) * 128], CT[:, h, n * 128:(n + 1) * 128], IDB,
                           reads=[("CT", h), "IDB"], writes=[PK(bank)])
                    cp("act" if cnt % 2 else "dve", VN[:, n, g * 1024:(g + 1) * 1024], psbf(bank),
                       reads=[PK(bank)], writes=htk)
            stmp = [(f32v(a_misc + mean[0].k[0][1] * 256, 1024), mean[0].k + mean[1].k),
                    (f32v(a_misc + rstd[0].k[0][1] * 256, 1024), rstd[0].k + rstd[1].k)]
            for h in range(16):
                b0 = 4 + 2 * (h % 2)
                for n in range(8):
                    bank = b0 + n // 4
                    mm(psb(bank)[:, (n % 4) * 128:(n % 4 + 1) * 128], VN[:, n, h * 128:(h + 1) * 128], WT[:, h, :],
                       True, True, reads=htk + WTb.k, writes=[PK(bank)])
                st, stk = stmp[h % 2]
                for q in range(2):
                    tt("dve", st[:, q * 512:(q + 1) * 512].rearrange("p (n t) -> p n t", n=4),
                       psb(b0 + q).rearrange("p (n t) -> p n t", n=4),
                       BS[:, h:h + 1, :].to_broadcast([128, 4, 128]), ALU.add,
                       reads=[PK(b0 + q)] + BSb.k, writes=stk)
                tt("pool", BT[:, h, :], st, BT[:, h, :], ALU.mult, reads=stk + [("BT", h)],
                   writes=[("BT", h)])
            dbg_at("gated", BT, "BT")
            linear(BT, "BT", dr["a_w_out"][0], 16, evac_resid, wide="CT")

        htk_all = [("HT", j) for j in range(16)]
        btk_all = [("BT", j) for j in range(16)]
        ctk_all = [("CT", j) for j in range(16)]

        def s5_prologue():
            po = [0]
            pkeys = []

            def P(words, bf=False):
                a = a_ha + po[0]
                po[0] += words
                assert po[0] <= 24576, po[0]
                k = [("pr", len(pkeys))]
                pkeys.append(k[0])
                return Buf(bf16v(a, words) if bf else f32v(a, words), k)

            V = lambda: P(64)
            LRE, LIM, LDT, DSK = V(), V(), P(8), P(16)
            dma("sp", LRE.ap, dr["b_lam_re"][0], "s5a", writes=LRE.k)
            dma("sp", LIM.ap, dr["b_lam_im"][0], "s5a", writes=LIM.k)
            dma("sp", LDT.ap[:, 0:1], dr["b_log_dt"][0].rearrange("(g o) -> g o", o=1), "s5a", writes=LDT.k)
            dma("sp", DSK.ap, dr["b_d"][0], "s5a", writes=DSK.k)
            BMR, BMI, CMR, CMI = P(1024), P(1024), P(1024), P(1024)
            dma("sp", BMR.ap, dr["b_bm_re"][0].rearrange("g p c -> g (p c)"), "s5b", writes=BMR.k)
            dma("sp", BMI.ap, dr["b_bm_im"][0].rearrange("g p c -> g (p c)"), "s5b", writes=BMI.k)
            dma("sp", CMR.ap, dr["b_cm_re"][0].rearrange("g c p -> g (c p)"), "s5b", writes=CMR.k)
            dma("sp", CMI.ap, dr["b_cm_im"][0].rearrange("g c p -> g (c p)"), "s5b", writes=CMI.k)
            S.close_group("s5a")
            S.close_group("s5b")
            dt, re_, rd, mag, ph, aa, cosv, sinv = P(8), V(), V(), V(), V(), V(), V(), V()
            L1r, L1i, nre, den, bfr, bfi, t1, t2 = V(), V(), V(), V(), V(), V(), V(), V()
            act(dt.ap[:, 0:1], LDT.ap[:, 0:1], AF.Exp, reads=LDT.k, writes=dt.k)
            ts("dve", re_.ap, LRE.ap, -1e-4, None, ALU.min, reads=LRE.k, writes=re_.k)
            ts("dve", rd.ap, re_.ap, dt.ap[:, 0:1], None, ALU.mult, reads=re_.k + dt.k, writes=rd.k)
            act(mag.ap, rd.ap, AF.Exp, reads=rd.k, writes=mag.k, scale=1.0 / 16)
            ts("dve", ph.ap, LIM.ap, dt.ap[:, 0:1], None, ALU.mult, reads=LIM.k + dt.k, writes=ph.k)
            act(sinv.ap, ph.ap, AF.Sin, reads=ph.k, writes=sinv.k, scale=1.0 / 16)
            ts("dve", aa.ap, ph.ap, 1.0 / 16, float(np.pi / 2), ALU.mult, ALU.add, reads=ph.k, writes=aa.k)
            act(cosv.ap, aa.ap, AF.Sin, reads=aa.k, writes=cosv.k)
            tt("dve", L1r.ap, mag.ap, cosv.ap, ALU.mult, reads=mag.k + cosv.k, writes=L1r.k)
            tt("dve", L1i.ap, mag.ap, sinv.ap, ALU.mult, reads=mag.k + sinv.k, writes=L1i.k)
            for _sq in range(4):
                tt("dve", t1.ap, L1r.ap, L1r.ap, ALU.mult, reads=L1r.k + t1.k, writes=t1.k)
                tt("dve", t2.ap, L1i.ap, L1i.ap, ALU.mult, reads=L1i.k + t2.k, writes=t2.k)
                tt("dve", den.ap, L1r.ap, L1i.ap, ALU.mult, reads=L1r.k + L1i.k + den.k, writes=den.k)
                tt("dve", L1r.ap, t1.ap, t2.ap, ALU.subtract, reads=t1.k + t2.k + den.k, writes=L1r.k)
                ts("dve", L1i.ap, den.ap, 2.0, None, ALU.mult, reads=den.k, writes=L1i.k)
            ts("dve", nre.ap, L1r.ap, -1.0, None, ALU.add, reads=L1r.k, writes=nre.k)
            tt("dve", den.ap, re_.ap, re_.ap, ALU.mult, reads=re_.k, writes=den.k)
            tt("dve", t1.ap, LIM.ap, LIM.ap, ALU.mult, reads=LIM.k, writes=t1.k)
            tt("dve", den.ap, den.ap, t1.ap, ALU.add, reads=den.k + t1.k, writes=den.k)
            S.add("dve", lambda e: e.reciprocal(out=den.ap, in_=den.ap), reads=den.k, writes=den.k)
            tt("dve", t1.ap, nre.ap, re_.ap, ALU.mult, reads=nre.k + re_.k + den.k, writes=t1.k)
            tt("dve", t2.ap, L1i.ap, LIM.ap, ALU.mult, reads=L1i.k + LIM.k, writes=t2.k)
            tt("dve", t1.ap, t1.ap, t2.ap, ALU.add, reads=t1.k + t2.k, writes=t1.k)
            tt("dve", bfr.ap, t1.ap, den.ap, ALU.mult, reads=t1.k + den.k, writes=bfr.k)
            tt("dve", t1.ap, L1i.ap, re_.ap, ALU.mult, reads=L1i.k + re_.k + bfr.k, writes=t1.k)
            tt("dve", t2.ap, nre.ap, LIM.ap, ALU.mult, reads=nre.k + LIM.k + t1.k, writes=t2.k)
            tt("dve", t1.ap, t1.ap, t2.ap, ALU.subtract, reads=t1.k + t2.k, writes=t1.k)
            tt("dve", bfi.ap, t1.ap, den.ap, ALU.mult, reads=t1.k + den.k, writes=bfi.k)

            def bc16(v):
                return v.unsqueeze(1).to_broadcast([128, 16, 64])

            def v3(b):
                return b.ap.rearrange("g (a b) -> g a b", a=16)

            def cmul3(eng, o_r, o_i, x_r, x_i, y_r, y_i, ta, tb, rk, wk):
                tt(eng, v3(ta), x_r, y_r, ALU.mult, reads=rk + wk, writes=ta.k)
                tt(eng, v3(tb), x_i, y_i, ALU.mult, reads=rk, writes=tb.k)
                tt(eng, o_r, v3(ta), v3(tb), ALU.subtract, reads=ta.k + tb.k, writes=wk)
                tt(eng, v3(ta), x_r, y_i, ALU.mult, reads=rk + wk, writes=ta.k)
                tt(eng, v3(tb), x_i, y_r, ALU.mult, reads=rk, writes=tb.k)
                tt(eng, o_i, v3(ta), v3(tb), ALU.add, reads=ta.k + tb.k, writes=wk)

            TA, TB, TC, TD = P(1024), P(1024), P(1024), P(1024)
            BBR, BBI = P(1024), P(1024)
            BMRv = BMR.ap.rearrange("g (p c) -> g c p", c=16)
            BMIv = BMI.ap.rearrange("g (p c) -> g c p", c=16)
            cmul3("dve", v3(BBR), v3(BBI), BMRv, BMIv, bc16(bfr.ap), bc16(bfi.ap), TA, TB,
                  BMR.k + BMI.k + bfr.k + bfi.k, BBR.k + BBI.k)
            PW = P(9 * 2 * 64)
            PWv = PW.ap.rearrange("g (k r p) -> g k r p", k=9, r=2)
            memset("dve", PWv[:, 0, 0, :], 1.0, writes=PW.k)
            memset("dve", PWv[:, 0, 1, :], 0.0, writes=PW.k)

            def cmulv(o_r, o_i, x_r, x_i, y_r, y_i, rk, wk):
                tt("dve", t1.ap, x_r, y_r, ALU.mult, reads=rk + wk + t1.k, writes=t1.k)
                tt("dve", t2.ap, x_i, y_i, ALU.mult, reads=rk + t2.k, writes=t2.k)
                tt("dve", nre.ap, x_r, y_i, ALU.mult, reads=rk + nre.k, writes=nre.k)
                tt("dve", den.ap, x_i, y_r, ALU.mult, reads=rk + den.k, writes=den.k)
                tt("dve", o_r, t1.ap, t2.ap, ALU.subtract, reads=t1.k + t2.k, writes=wk)
                tt("dve", o_i, nre.ap, den.ap, ALU.add, reads=nre.k + den.k, writes=wk)

            for k in range(1, 9):
                cmulv(PWv[:, k, 0, :], PWv[:, k, 1, :], PWv[:, k - 1, 0, :], PWv[:, k - 1, 1, :], L1r.ap, L1i.ap,
                      L1r.k + L1i.k, PW.k)
            PL = P(4 * 64)
            PLv = PL.ap.rearrange("g (k p) -> g k p", k=4)
            SQ_ = P(2 * 64)
            SQv = SQ_.ap.rearrange("g (k p) -> g k p", k=2)
            cur_r, cur_i, curk = PWv[:, 8, 0, :], PWv[:, 8, 1, :], PW.k
            for it_ in range(7):
                if it_ % 2 == 0:
                    o_r, o_i, ok = SQv[:, 0, :], SQv[:, 1, :], SQ_.k
                else:
                    o_r, o_i, ok = PLv[:, 2, :], PLv[:, 3, :], PL.k
                if it_ == 6:
                    o_r, o_i, ok = PLv[:, 0, :], PLv[:, 1, :], PL.k
                cmulv(o_r, o_i, cur_r, cur_i, cur_r, cur_i, curk, ok)
                cur_r, cur_i, curk = o_r, o_i, ok
            cmulv(PLv[:, 2, :], PLv[:, 3, :], PLv[:, 0, :], PLv[:, 1, :], PLv[:, 0, :], PLv[:, 1, :], PL.k, PL.k)
            Tq = P(128)
            srcs6 = [(PWv[:, 8, 0, :], PW.k), (PWv[:, 8, 1, :], PW.k), (PLv[:, 0, :], PL.k), (PLv[:, 1, :], PL.k),
                     (PLv[:, 2, :], PL.k), (PLv[:, 3, :], PL.k)]
            for k6, (src, sk) in enumerate(srcs6):
                b1, b2 = 4 + k6 % 2, 6 + k6 % 2
                tr(psb(b1)[0:64, 0:128], src, IDF, reads=sk + ["IDF"], writes=[PK(b1)])
                cp("dve", Tq.ap[0:64, :], psb(b1)[0:64, 0:128], reads=[PK(b1)], writes=Tq.k)
                Tq2 = Tq.ap[0:64, :].rearrange("p (q l) -> p q l", l=2)
                mm(psb(b2)[:, 0:64], IDF[0:64, :], Tq2[:, :, 0], True, False, reads=Tq.k + ["IDF"], writes=[PK(b2)])
                mm(psb(b2)[:, 0:64], SELB[0:64, :], Tq2[:, :, 1], False, True, reads=Tq.k + ["SELB"], writes=[PK(b2)])
                cp("dve", COEF[:, k6, :], psb(b2)[:, 0:64], reads=[PK(b2)], writes=["COEF"])
            CLR, CLI = P(1024), P(1024)
            VR, VI = P(1024), P(1024)
            WXs = P(2048, bf=True)
            WCs = P(2048, bf=True)
            Ks = P(1024, bf=True)
            KTt = P(256)
            TE, TF = P(1024), P(1024)
            WXsv = WXs.ap.rearrange("g (c r x) -> g c r x", c=16, r=2)
            WCsv = WCs.ap.rearrange("g (p r x) -> g p r x", p=64, r=2)
            Ksv = Ks.ap.rearrange("g (c x) -> g c x", c=16)
            KTv = KTt.ap.rearrange("g (a b) -> g a b", a=16)
            WXDv = WXD.rearrange("(g c) (i x) -> g c i x", c=16, i=8)
            WCDv = WCD.rearrange("q (l p) (i x) -> (q l) p i x", l=2, i=8)
            KDv = KD.rearrange("(g c) (t x) -> g c t x", c=16, t=8)

            def emit_K(tau, XR, XI):
                for c in range(16):
                    eng, ta, tb = ("dve", TA, TB) if c % 2 == 0 else ("pool", TE, TF)
                    xr = XR.ap[:, c * 64:(c + 1) * 64].unsqueeze(1).to_broadcast([128, 16, 64])
                    xi = XI.ap[:, c * 64:(c + 1) * 64].unsqueeze(1).to_broadcast([128, 16, 64])
                    tt(eng, v3(ta), v3(BBR), xr, ALU.mult, reads=BBR.k + XR.k, writes=ta.k)
                    tt(eng, v3(tb), v3(BBI), xi, ALU.mult, reads=BBI.k + XI.k, writes=tb.k)
                    tt(eng, v3(ta), v3(ta), v3(tb), ALU.subtract, reads=ta.k + tb.k, writes=ta.k)
                    S.add("dve", lambda e, o=KTv[:, :, c], a=v3(ta): e.reduce_sum(out=o, in_=a, axis=AX.X),
                          reads=ta.k, writes=KTt.k)
                if tau == 0:
                    dg = KTt.ap.rearrange("g (a b) -> g a b", b=17 if False else 16)
                    for c in range(16):
                        tt("dve", KTv[:, c, c:c + 1], KTv[:, c, c:c + 1], DSK.ap[:, c:c + 1], ALU.add,
                           reads=KTt.k + DSK.k, writes=KTt.k)
                for m8 in range(8):
                    ts("dve", Ksv[:, :, m8 * 16:(m8 + 1) * 16], KTv, MK[:, 2 + m8:3 + m8], None, ALU.mult,
                       reads=KTt.k + ["MK"], writes=Ks.k)
                dma("sp", KDv[:, :, tau, :], Ksv, "s5ck", reads=Ks.k, writes=["KD"])

            emit_K(0, CMR, CMI)
            for i in range(8):
                cmul3("dve", v3(CLR), v3(CLI), v3(CMR), v3(CMI), bc16(PWv[:, i + 1, 0, :]), bc16(PWv[:, i + 1, 1, :]),
                      TA, TB, CMR.k + CMI.k + PW.k, CLR.k + CLI.k)
                CLRt = CLR.ap.rearrange("g (c p) -> g p c", c=16)
                CLIt = CLI.ap.rearrange("g (c p) -> g p c", c=16)
                for m2 in range(2):
                    ts("pool", WCsv[:, :, 0, m2 * 16:(m2 + 1) * 16], CLRt, MK[:, m2:m2 + 1], None, ALU.mult,
                       reads=CLR.k + ["MK"], writes=WCs.k)
                    ts("pool", WCsv[:, :, 1, m2 * 16:(m2 + 1) * 16], CLIt, MK[:, 10 + m2:11 + m2], None, ALU.mult,
                       reads=CLI.k + ["MK"], writes=WCs.k)
                dma("sp", WCDv[:, :, i, :], WCs.ap.rearrange("g (p x) -> g p x", p=64), "s5cw", reads=WCs.k,
                    writes=["WCD"])
                cmul3("dve", v3(VR), v3(VI), v3(BBR), v3(BBI), bc16(PWv[:, 7 - i, 0, :]), bc16(PWv[:, 7 - i, 1, :]),
                      TC, TD, BBR.k + BBI.k + PW.k, VR.k + VI.k)
                for m2 in range(2):
                    ts("pool", WXsv[:, :, 0, m2 * 64:(m2 + 1) * 64], v3(VR), MK[:, m2:m2 + 1], None, ALU.mult,
                       reads=VR.k + ["MK"], writes=WXs.k)
                    ts("pool", WXsv[:, :, 1, m2 * 64:(m2 + 1) * 64], v3(VI), MK[:, m2:m2 + 1], None, ALU.mult,
                       reads=VI.k + ["MK"], writes=WXs.k)
                dma("sp", WXDv[:, :, i, :], WXs.ap.rearrange("g (c x) -> g c x", c=16), "s5cx", reads=WXs.k,
                    writes=["WXD"])
                if i < 7:
                    emit_K(i + 1, CLR, CLI)
            allp = [k for k in pkeys]
            memset("dve", CST[:, 7:8], 0.0, writes=allp + htk_all + btk_all + ctk_all)
            memset("pool", CST[:, 6:7], 0.0, writes=allp + htk_all + btk_all + ctk_all)

        def s5_mixer(use_cc):
            rmsnorm_to(HT, "HT", "g_mix1")

            def evac_u(m_, half, bank):
                cp("act", BT[:, m_, half * 512:(half + 1) * 512], psb(bank), reads=[PK(bank)], writes=[("BT", m_)])

            linear(HT, "HT", dr["b_w_in"][0], 16, evac_u, wide="CT")
            dbg_at("s5u", BT, "BT")
            m = Misc()
            WXb = [m.bf16(2048), m.bf16(2048)]
            Kb = [m.bf16(1024), m.bf16(1024)]
            WCb = [m.bf16(2048), m.bf16(2048)]
            SC = m.f32(1024)
            sck = []

            def sub(o, w, nm):
                sck.append(("sc", nm))
                return Buf(SC.ap[:, o:o + w], [("sc", nm)])

            AAb, BBb = sub(0, 128, "aa"), sub(128, 128, "bb")
            S3 = [sub(256, 192, "s3a"), sub(448, 192, "s3b")]
            U1, U2 = sub(640, 128, "u1"), sub(768, 128, "u2")
            W = sub(896, 128, "w")
            memset("dve", SC.ap, 0.0, writes=SC.k + sck)
            XLq = CT.rearrange("p j t -> p (j t)").rearrange("p (q r n) -> p q r n", q=64, r=2)
            XLn = CT.rearrange("p j t -> p (j t)").rearrange("p (q r n) -> p r q n", q=64, r=2)
            mb = [4]

            def next_mb():
                b = 4 + mb[0] % 4
                mb[0] += 1
                return b

            for j in range(16):
                wx = WXb[j % 2]
                dma("sp", wx.ap, WXD[j * 128:(j + 1) * 128, :], "s5x%d" % (j % 2), reads=["WXD"], writes=wx.k)
                wxv = wx.ap.rearrange("p (i r x) -> p i r x", i=8, r=2)
                ub = BT[:, j, :].rearrange("p (n i) -> p n i", i=8)
                for q4 in range(4):
                    bank = next_mb()
                    rows = slice(32 * q4, 32 * q4 + 32)
                    for part in range(2):
                        for i in range(8):
                            mm(psb(bank)[:, part * 128:(part + 1) * 128], wxv[rows, i, part, :], ub[rows, :, i],
                               i == 0, i == 7, reads=wx.k + [("BT", j)], writes=[PK(bank)], tile_position=(32 * q4, 0))
                    cp("act" if q4 % 2 else "dve", XLq[:, 4 * j + q4, :, :],
                       psb(bank)[:, 0:256].rearrange("p (r n) -> p r n", r=2), reads=[PK(bank)], writes=[("CT", j)])
            AA = AAb.ap.rearrange("p (r q) -> p r q", r=2)
            BB = BBb.ap.rearrange("p (r q) -> p r q", r=2)
            cp("dve", AA[:, 0, :], COEF[:, 0, :], reads=["COEF"], writes=AAb.k)
            cp("dve", AA[:, 1, :], COEF[:, 0, :], reads=["COEF"], writes=AAb.k)
            ts("dve", BB[:, 0, :], COEF[:, 1, :], -1.0, None, ALU.mult, reads=["COEF"], writes=BBb.k)
            cp("dve", BB[:, 1, :], COEF[:, 1, :], reads=["COEF"], writes=BBb.k)
            s3v = [b.ap.rearrange("p (r q) -> p r q", r=3) for b in S3]
            u1v = U1.ap.rearrange("p (r q) -> p r q", r=2)
            u2v = U2.ap.rearrange("p (r q) -> p r q", r=2)

            def scan(init_zero, write_z):
                cur = 0
                if init_zero:
                    memset("dve", S3[0].ap, 0.0, writes=S3[0].k)
                for n in range(NBLK):
                    o, nw = s3v[cur], s3v[1 - cur]
                    ok, nk = S3[cur].k, S3[1 - cur].k
                    tt("dve", u1v, AA, o[:, 0:2, :], ALU.mult, reads=AAb.k + ok, writes=U1.k)
                    tt("dve", u2v, BB, o[:, 1:3, :], ALU.mult, reads=BBb.k + ok, writes=U2.k)
                    tt("dve", u1v, u1v, u2v, ALU.add, reads=U1.k + U2.k, writes=U1.k)
                    tt("dve", nw[:, 0:2, :], u1v, XLn[:, :, :, n], ALU.add, reads=U1.k + ctk_all, writes=nk)
                    cp("dve", nw[:, 2, :], nw[:, 0, :], reads=nk, writes=nk)
                    if write_z:
                        cp("dve", XLn[:, :, :, n], o[:, 0:2, :], reads=ok, writes=ctk_all)
                    cur = 1 - cur
                return cur

            if use_cc:
                fin = scan(True, False)
                dma("sp", XB[:, :], S3[fin].ap[:, 0:128], "misc3", reads=S3[fin].k, writes=["XB"])
                S.add("pool", lambda e: e.collective_compute("AllGather", op=ALU.bypass,
                                                             replica_groups=[list(range(8))],
                                                             ins=[XB_t.ap().opt()], outs=[GATH_t.ap().opt()]),
                      reads=["XB"], writes=["GATH"], dma_sem="cc")
                G8 = f32v(a_ha, 1024).rearrange("p (r k q) -> p r k q", r=8, k=2)
                dma("sp", f32v(a_ha, 1024).rearrange("p (r x) -> p r x", r=8),
                    GATH.rearrange("(r p) x -> p r x", p=128), "misc3", reads=["GATH"], writes=htk_all)
                zin = s3v[0]
                Wv = W.ap.rearrange("p (r q) -> p r q", r=2)
                memset("dve", S3[0].ap, 0.0, writes=S3[0].k)
                for r8 in range(8):
                    s0, s1, s2 = (SEL[:, r8 * 3 + k:r8 * 3 + k + 1] for k in range(3))
                    ts("dve", Wv, COEF[:, 2:4, :], s1, None, ALU.mult, reads=["COEF", "SEL"] + W.k, writes=W.k)
                    stt("dve", Wv, COEF[:, 4:6, :], s2, Wv, ALU.mult, ALU.add, reads=["COEF", "SEL"] + W.k, writes=W.k)
                    ts("dve", Wv[:, 0, :], Wv[:, 0, :], s0, None, ALU.add, reads=W.k, writes=W.k)
                    g = G8[:, r8]
                    tt("dve", u1v[:, 0, :], Wv[:, 0, :], g[:, 0, :], ALU.mult, reads=W.k + htk_all + U1.k, writes=U1.k)
                    tt("dve", u1v[:, 1, :], Wv[:, 0, :], g[:, 1, :], ALU.mult, reads=W.k + htk_all, writes=U1.k)
                    tt("dve", u2v[:, 0, :], Wv[:, 1, :], g[:, 1, :], ALU.mult, reads=W.k + htk_all + U2.k, writes=U2.k)
                    tt("dve", u2v[:, 1, :], Wv[:, 1, :], g[:, 0, :], ALU.mult, reads=W.k + htk_all, writes=U2.k)
                    tt("dve", zin[:, 0, :], zin[:, 0, :], u1v[:, 0, :], ALU.add, reads=U1.k + S3[0].k, writes=S3[0].k)
                    tt("dve", zin[:, 0, :], zin[:, 0, :], u2v[:, 0, :], ALU.subtract, reads=U2.k + S3[0].k, writes=S3[0].k)
                    tt("dve", zin[:, 1, :], zin[:, 1, :], u1v[:, 1, :], ALU.add, reads=U1.k + S3[0].k, writes=S3[0].k)
                    tt("dve", zin[:, 1, :], zin[:, 1, :], u2v[:, 1, :], ALU.add, reads=U2.k + S3[0].k, writes=S3[0].k)
                cp("dve", zin[:, 2, :], zin[:, 0, :], reads=S3[0].k, writes=S3[0].k)
                scan(False, True)
            else:
                scan(True, True)
            for j in range(16):
                kb, wc = Kb[j % 2], WCb[j % 2]
                dma("sp", kb.ap, KD[j * 128:(j + 1) * 128, :], "s5k%d" % (j % 2), reads=["KD"], writes=kb.k)
                dma("sp", wc.ap.rearrange("p (q x) -> p q x", q=4), WCD[4 * j:4 * j + 4].rearrange("q r x -> r q x"),
                    "s5w%d" % (j % 2), reads=["WCD"], writes=wc.k)
                kv = kb.ap.rearrange("p (t x) -> p t x", t=8)
                wcv = wc.ap.rearrange("p (q i r x) -> p q i r x", q=4, i=8, r=2)
                for hb in range(2):
                    bank = next_mb()
                    pv = psb(bank).rearrange("p (n i) -> p n i", i=8)
                    uv = BT[:, j, hb * 512:(hb + 1) * 512].rearrange("p (n i) -> p n i", i=8)
                    for tau in range(8):
                        mm(pv[:, :, tau:8], kv[:, tau, :], uv[:, :, 0:8 - tau], tau == 0, False,
                           reads=kb.k + [("BT", j)], writes=[PK(bank)], skip_group_check=True)
                    for q4 in range(4):
                        rows = slice(32 * q4, 32 * q4 + 32)
                        for i in range(8):
                            for part in range(2):
                                last = (q4 == 3 and i == 7 and part == 1)
                                mm(pv[rows, :, i], wcv[:, q4, i, part, :], XLq[:, 4 * j + q4, part, hb * 64:(hb + 1) * 64],
                                   False, last, reads=wc.k + [("CT", j)], writes=[PK(bank)],
                                   tile_position=(0, 32 * q4), skip_group_check=True)
                    act(HT[:, j, hb * 512:(hb + 1) * 512], psb(bank), AF.Gelu, reads=[PK(bank)], writes=[("HT", j)])
            memset("dve", CST[:, 5:6], 0.0, writes=SC.k + sck)
            dbg_at("s5y", HT, "HT")
            GT = [f32v(a_hb + k * 512, 512) for k in range(8)]
            wo = dr["b_w_out"][0]
            for blk in range(4):
                def evac_gate(m_, half, bank, blk=blk):
                    mo = blk * 4 + m_
                    act(GT[m_ * 2 + half], psb(bank), AF.Sigmoid, reads=[PK(bank), "PAR"], writes=btk_all,
                        bias=pap("b_b_gate", mo), scale=1.0)

                def evac_val(m_, half, bank, blk=blk):
                    mo = blk * 4 + m_
                    g = GT[m_ * 2 + half]
                    stt("dve", g, psb(bank), pap("b_b_val", mo), g, ALU.add, ALU.mult,
                        reads=[PK(bank), "PAR"] + btk_all, writes=btk_all)
                    tt("dve", XT[:, mo, half * 512:(half + 1) * 512], XT[:, mo, half * 512:(half + 1) * 512], g,
                       ALU.add, reads=btk_all + [("XT", mo)], writes=[("XT", mo)])

                linear(HT, "HT", wo, 4, evac_gate, col0=D + blk * 512, wide="CT")
                linear(HT, "HT", wo, 4, evac_val, col0=blk * 512, wide="CT")

        def final_store(apply_norm=True):
            m = Misc()
            if apply_norm:
                rstd = rms_stats(m, lambda j, h: XT[:, j, h * 512:(h + 1) * 512], lambda j: ("XT", j), "fn")
                for half in range(2):
                    for j in range(16):
                        stt("dve", XT[:, j, half * 512:(half + 1) * 512], XT[:, j, half * 512:(half + 1) * 512],
                            pap("g_final", j), rstd[half].ap, ALU.mult, ALU.mult,
                            reads=[("XT", j), "PAR"] + rstd[half].k, writes=[("XT", j)])
            ot = [m.f32(2048), m.f32(2048)]
            for tt_ in range(8):
                b = tt_ % 2
                for g4 in range(4):
                    bank = 4 + (g4 % 2) + 2 * (tt_ % 2)
                    for jj in range(4):
                        j = g4 * 4 + jj
                        tr(psb(bank)[:, jj * 128:(jj + 1) * 128], XT[:, j, tt_ * 128:(tt_ + 1) * 128], IDF,
                           reads=[("XT", j), "IDF"], writes=[PK(bank)])
                    cp("dve" if g4 % 2 == 0 else "act", ot[b].ap[:, g4 * 512:(g4 + 1) * 512], psb(bank),
                       reads=[PK(bank)], writes=ot[b].k)
                dma("sp", out_d[tt_ * 128:(tt_ + 1) * 128, :], ot[b].ap, "xout%d" % b, reads=ot[b].k,
                    writes=[("outd", tt_)])
            S.add("sp", lambda e: e.wait_ge(sems["xout0"], S.dma_counts["xout0"]),
                  reads=[("outd", t) for t in range(8)])
            S.add("sp", lambda e: e.wait_ge(sems["xout1"], S.dma_counts["xout1"]))

        def dump_fm(buf, name):
            for j in range(16):
                cp("dve", XT[:, j, :], buf[:, j, :], reads=[(name, j)], writes=[("XT", j)])

        class _Stop(Exception):
            pass

        def dbg_at(tag, buf, name):
            if DBG == tag:
                dump_fm(buf, name)
                raise _Stop()

        if stage >= 4 or stage < 0:
            s5_prologue()
        load_x()
        try:
            if stage >= 1:
                gmlp()
            if stage >= 2:
                xattn(0)
            if stage >= 3:
                ffn(0)
            if stage >= 4 or stage < 0:
                s5_mixer(use_cc=(ncores == 8))
            if stage >= 5:
                xattn(1)
            if stage >= 6:
                ffn(1)
            final_store(apply_norm=(stage >= 9 or stage == 0))
        except _Stop:
            final_store(apply_norm=False)

        with nc.Block() as block:
            engs = {}

            @block.tensor
            def _(e):
                engs["pe"] = e
                S.emit_all_one(nc, "pe", e, sems)

            @block.vector
            def _(e):
                S.emit_all_one(nc, "dve", e, sems)

            @block.scalar
            def _(e):
                S.emit_all_one(nc, "act", e, sems)

            @block.gpsimd
            def _(e):
                S.emit_all_one(nc, "pool", e, sems)

            @block.sync
            def _(e):
                S.emit_all_one(nc, "sp", e, sems)
    return nc


def _emit_all_one(self, nc, ename, eng, sems):
    if not getattr(self, "_assigned", False):
        for e in self.ENGS:
            c = 0
            for op in self.ops[e]:
                if op.signal and op.dma_sem is None:
                    c += 1
                    op.sigval = c
        self._assigned = True
    for op in self.ops[ename]:
        for w in op.waits:
            if w[0] == "op":
                src = w[1]
                if src.dma_sem is not None:
                    eng.wait_ge(sems[src.dma_sem], src.dma_val)
                else:
                    eng.wait_ge(sems["eng_" + src.eng], src.sigval)
            else:
                eng.wait_ge(sems[w[1]], w[2])
        ins = op.emit(eng)
        if op.dma_sem == "cc":
            ins.then_inc(sems["cc"])
        elif op.dma_sem is not None:
            ins.then_inc(sems[op.dma_sem], 16)
        elif op.signal:
            ins.then_inc(sems["eng_" + ename], 1)


Sched.emit_all_one = _emit_all_one


def make_in_maps(inputs, ncores=8):
    ident = np.eye(128, dtype=np.float32)
    tril = np.tril(np.ones((128, 128), dtype=np.float32))
    xs = np.ascontiguousarray(inputs["x"], dtype=np.float32).reshape(8, NTOK, D)
    mem = np.ascontiguousarray(inputs["mem"], dtype=np.float32)
    maps = []
    for c in range(ncores):
        r = c % 4
        sel = np.zeros((128, 24), dtype=np.float32)
        for j in range(4):
            if j < r:
                sel[:, (4 * (c // 4) + j) * 3 + (r - 1 - j)] = 1.0
        mk = np.zeros((128, 12), dtype=np.float32)
        gidx = np.arange(128)
        for m_ in range(2):
            mk[:, m_] = (gidx % 2 == m_)
            mk[:, 10 + m_] = -1.0 * (gidx % 2 == m_)
        for m_ in range(8):
            mk[:, 2 + m_] = (gidx % 8 == m_)
        selb = np.concatenate([np.zeros((64, 64), np.float32), np.eye(64, dtype=np.float32)], axis=1)
        d = {"x": xs[c], "mem": mem[c // 4], "c_ident": ident, "c_tril": tril, "c_sel": sel, "c_mk": mk,
             "c_selb": selb}
        for name, _ in PARAM_SPECS:
            if name not in d:
                d[name] = np.ascontiguousarray(inputs[name], dtype=np.float32)
        maps.append(d)
    return maps


def kernel(**inputs):
    nc = build_program()
    maps = make_in_maps(inputs)
    res = run_bass_kernel_spmd(nc, maps, core_ids=list(range(8)))
    out = np.concatenate([np.asarray(r["out"], dtype=np.float32) for r in res.results], axis=0)
    return out.reshape(2, 4096, D)
```
